# Optimizing a Trainium2 kernel written in Bass

```python
import jax, jax.numpy as jnp
from jax import lax
import numpy as np

D_MODEL = 1024
BATCH = 2
SEQ = 8192
DEPTH = 2
DEC_BATCH = 16
DEC_SEQ = 32
PAST_LEN = 2048

CHUNK = 64
NORM_EPS = 1e-6

SSD_EXPAND = 2
SSD_D_INNER = SSD_EXPAND * D_MODEL
SSD_HEAD_DIM = 64
SSD_N_HEADS = SSD_D_INNER // SSD_HEAD_DIM
SSD_N_GROUPS = 4
SSD_HEADS_PER_GROUP = SSD_N_HEADS // SSD_N_GROUPS
SSD_D_STATE = 128
SSD_CONV_W = 4
SSD_CONV_DIM = SSD_D_INNER + 2 * SSD_N_GROUPS * SSD_D_STATE
SSD_IN_DIM = SSD_D_INNER + SSD_CONV_DIM + SSD_N_HEADS
SSD_NORM_GROUP = SSD_D_INNER // SSD_N_GROUPS
SSD_DT_MIN = 1e-3
SSD_DT_MAX = 1e-1

HGRN_DIM = D_MODEL
HGRN_HEAD_DIM = 128
HGRN_N_HEADS = HGRN_DIM // HGRN_HEAD_DIM

FFN_HIDDEN = -(-(8 * D_MODEL) // (3 * 256)) * 256

N_SSD_LAYERS = (DEPTH + 1) // 2
N_HGRN_LAYERS = DEPTH // 2

kernel_name = "hybrid_ssd_hgrn2_streaming_step"


def rmsnorm(x, w):
    xf = x.astype(jnp.float32)
    y = xf * lax.rsqrt(jnp.mean(xf * xf, axis=-1, keepdims=True) + NORM_EPS)
    return (y * w.astype(jnp.float32)).astype(x.dtype)


def _chunkify(a, n_chunks):
    pad = n_chunks * CHUNK - a.shape[1]
    a = jnp.pad(a, [(0, 0), (0, pad)] + [(0, 0)] * (a.ndim - 2))
    a = a.reshape((a.shape[0], n_chunks, CHUNK) + a.shape[2:])
    return jnp.moveaxis(a, 1, 0)


def _unchunkify(a, length):
    a = jnp.moveaxis(a, 0, 1)
    a = a.reshape((a.shape[0], -1) + a.shape[3:])
    return a[:, :length]


def ssd_scan(x, dt, A, Bm, Cm, S0):
    Bsz, L = x.shape[0], x.shape[1]
    G, Hg, P, N = SSD_N_GROUPS, SSD_HEADS_PER_GROUP, SSD_HEAD_DIM, SSD_D_STATE
    n = -(-L // CHUNK)
    xs = _chunkify(x.reshape(Bsz, L, G, Hg, P), n)
    dts = _chunkify(dt.reshape(Bsz, L, G, Hg), n)
    Bs = _chunkify(Bm, n)
    Cs = _chunkify(Cm, n)
    Ag = A.reshape(G, Hg)
    mask = jnp.tril(jnp.ones((CHUNK, CHUNK), dtype=bool))[None, :, :, None, None]

    def step(S, inp):
        xc, dtc, Bc, Cc = inp
        cum = jnp.cumsum(dtc * Ag, axis=1)
        seg = cum[:, :, None] - cum[:, None, :]
        decay = jnp.where(mask, jnp.exp(jnp.where(mask, seg, 0.0)), 0.0)
        xdt = xc * dtc[..., None]
        cb = jnp.einsum('btgn,bsgn->btsg', Cc, Bc)
        y = jnp.einsum('btsg,btsgh,bsghp->btghp', cb, decay, xdt)
        y = y + jnp.einsum('btgn,bghpn->btghp', Cc, S) * jnp.exp(cum)[..., None]
        last = cum[:, -1]
        w_end = jnp.exp(last[:, None] - cum)
        S = jnp.exp(last)[..., None, None] * S + jnp.einsum('bsgh,bsghp,bsgn->bghpn', w_end, xdt, Bc)
        return S, y

    S, ys = lax.scan(step, S0.reshape(Bsz, G, Hg, P, N), (xs, dts, Bs, Cs))
    y = _unchunkify(ys, L).reshape(Bsz, L, G * Hg, P)
    return y, S.reshape(Bsz, G * Hg, P, N)


def gla_scan(q, k, v, logg, S0):
    L = q.shape[1]
    n = -(-L // CHUNK)
    qs, ks, vs, gs = (_chunkify(a, n) for a in (q, k, v, logg))
    mask = jnp.tril(jnp.ones((CHUNK, CHUNK), dtype=bool))[None, None]

    def step(S, inp):
        qc, kc, vc, gc = inp
        b = jnp.cumsum(gc, axis=1)
        qe = qc * jnp.exp(b)
        ke = kc * jnp.exp(-b)
        sc = jnp.where(mask, jnp.einsum('bthk,bshk->bhts', qe, ke), 0.0)
        o = jnp.einsum('bhts,bshv->bthv', sc, vc) + jnp.einsum('bthk,bhkv->bthv', qe, S)
        last = b[:, -1]
        S = jnp.exp(last)[..., None] * S + jnp.einsum('bshk,bshv->bhkv', kc * jnp.exp(last[:, None] - b), vc)
        return S, o

    S, os_ = lax.scan(step, S0, (qs, ks, vs, gs))
    return _unchunkify(os_, L), S


def ssd_mixer(h, conv_buf, ssm_state, in_w, conv_w, conv_b, dt_bias, A_log, D_skip, gnorm_w, out_w):
    Bsz, L, _ = h.shape
    proj = h @ in_w
    z = proj[..., :SSD_D_INNER]
    xbc = proj[..., SSD_D_INNER:SSD_D_INNER + SSD_CONV_DIM]
    dt_raw = proj[..., SSD_D_INNER + SSD_CONV_DIM:]
    xpad = jnp.concatenate([conv_buf.astype(xbc.dtype), xbc], axis=1)
    new_conv = xpad[:, -(SSD_CONV_W - 1):]
    acc = conv_b.astype(jnp.float32)
    for tap in range(SSD_CONV_W):
        acc = acc + xpad[:, tap:tap + L].astype(jnp.float32) * conv_w[tap].astype(jnp.float32)
    xbc = jax.nn.silu(acc)
    xs = xbc[..., :SSD_D_INNER].reshape(Bsz, L, SSD_N_HEADS, SSD_HEAD_DIM)
    Bm = xbc[..., SSD_D_INNER:SSD_D_INNER + SSD_N_GROUPS * SSD_D_STATE].reshape(Bsz, L, SSD_N_GROUPS, SSD_D_STATE)
    Cm = xbc[..., SSD_D_INNER + SSD_N_GROUPS * SSD_D_STATE:].reshape(Bsz, L, SSD_N_GROUPS, SSD_D_STATE)
    dt = jax.nn.softplus(dt_raw.astype(jnp.float32) + dt_bias.astype(jnp.float32))
    A = -jnp.exp(A_log.astype(jnp.float32))
    y, S = ssd_scan(xs, dt, A, Bm, Cm, ssm_state.astype(jnp.float32))
    y = y + D_skip.astype(jnp.float32)[:, None] * xs
    yg = y.reshape(Bsz, L, SSD_D_INNER) * jax.nn.silu(z.astype(jnp.float32))
    yg = yg.reshape(Bsz, L, SSD_N_GROUPS, SSD_NORM_GROUP)
    yg = yg * lax.rsqrt(jnp.mean(yg * yg, axis=-1, keepdims=True) + NORM_EPS)
    yg = yg.reshape(Bsz, L, SSD_D_INNER) * gnorm_w.astype(jnp.float32)
    out = yg.astype(h.dtype) @ out_w
    return out, new_conv, S


def hgrn_mixer(h, state, lower_bound, in_w, gnorm_w, out_w):
    Bsz, L, _ = h.shape
    proj = h @ in_w
    q, f, i, g = jnp.split(proj.astype(jnp.float32), 4, axis=-1)
    heads = (Bsz, L, HGRN_N_HEADS, HGRN_HEAD_DIM)
    q = jax.nn.silu(q).reshape(heads)
    lb = lower_bound.astype(jnp.float32).reshape(HGRN_N_HEADS, HGRN_HEAD_DIM)
    forget = lb + (1.0 - lb) * jax.nn.sigmoid(f.reshape(heads))
    k = 1.0 - forget
    logg = jnp.log(forget)
    o, S = gla_scan(q, k, i.reshape(heads), logg, state.astype(jnp.float32))
    o = o * lax.rsqrt(jnp.mean(o * o, axis=-1, keepdims=True) + NORM_EPS) * gnorm_w.astype(jnp.float32)
    o = o.reshape(Bsz, L, HGRN_DIM) * jax.nn.silu(g)
    out = o.astype(h.dtype) @ out_w
    return out, S


def swiglu(h, w_gate, w_up, w_down):
    return (jax.nn.silu(h @ w_gate) * (h @ w_up)) @ w_down


def setup_inputs(seed: int = 0) -> dict:
    key = jax.random.key(seed)
    ks = jax.random.split(key, 24)
    f32 = jnp.float32

    def nrm(k, shape, scale):
        return jax.random.normal(k, shape, f32) * scale

    u = jax.random.uniform(ks[8], (N_SSD_LAYERS, SSD_N_HEADS), f32)
    dt0 = jnp.exp(u * (np.log(SSD_DT_MAX) - np.log(SSD_DT_MIN)) + np.log(SSD_DT_MIN))
    dt_bias = dt0 + jnp.log(-jnp.expm1(-dt0))
    A_log = jnp.log(jax.random.uniform(ks[9], (N_SSD_LAYERS, SSD_N_HEADS), f32, 1.0, 16.0))
    return {
        "x_prompt": nrm(ks[0], (BATCH, SEQ, D_MODEL), 1.0),
        "x_sample": nrm(ks[1], (DEC_BATCH, DEC_SEQ, D_MODEL), 1.0),
        "state_ssd": nrm(ks[2], (N_SSD_LAYERS, DEC_BATCH, SSD_N_HEADS, SSD_HEAD_DIM, SSD_D_STATE), 0.5),
        "cache_conv": nrm(ks[3], (N_SSD_LAYERS, DEC_BATCH, SSD_CONV_W - 1, SSD_CONV_DIM), 1.0),
        "state_hgrn": nrm(ks[4], (N_HGRN_LAYERS, DEC_BATCH, HGRN_N_HEADS, HGRN_HEAD_DIM, HGRN_HEAD_DIM), 0.5),
        "ssd_norm_w": 1.0 + nrm(ks[5], (N_SSD_LAYERS, D_MODEL), 0.02),
        "ssd_in_w": nrm(ks[6], (N_SSD_LAYERS, D_MODEL, SSD_IN_DIM), D_MODEL ** -0.5),
        "ssd_conv_w": nrm(ks[7], (N_SSD_LAYERS, SSD_CONV_W, SSD_CONV_DIM), SSD_CONV_W ** -0.5),
        "ssd_conv_b": nrm(ks[10], (N_SSD_LAYERS, SSD_CONV_DIM), 0.02),
        "ssd_dt_bias": dt_bias,
        "ssd_A_log": A_log,
        "ssd_D": 1.0 + nrm(ks[11], (N_SSD_LAYERS, SSD_N_HEADS), 0.1),
        "ssd_gnorm_w": 1.0 + nrm(ks[12], (N_SSD_LAYERS, SSD_D_INNER), 0.02),
        "ssd_out_w": nrm(ks[13], (N_SSD_LAYERS, SSD_D_INNER, D_MODEL), SSD_D_INNER ** -0.5),
        "hgrn_norm_w": 1.0 + nrm(ks[14], (N_HGRN_LAYERS, D_MODEL), 0.02),
        "hgrn_in_w": nrm(ks[15], (N_HGRN_LAYERS, D_MODEL, 4 * HGRN_DIM), D_MODEL ** -0.5),
        "hgrn_lower_bounds": nrm(ks[16], (DEPTH, HGRN_DIM), 0.1),
        "hgrn_gnorm_w": 1.0 + nrm(ks[17], (N_HGRN_LAYERS, HGRN_HEAD_DIM), 0.02),
        "hgrn_out_w": nrm(ks[18], (N_HGRN_LAYERS, HGRN_DIM, D_MODEL), HGRN_DIM ** -0.5),
        "ffn_norm_w": 1.0 + nrm(ks[19], (DEPTH, D_MODEL), 0.02),
        "ffn_w_gate": nrm(ks[20], (DEPTH, D_MODEL, FFN_HIDDEN), D_MODEL ** -0.5),
        "ffn_w_up": nrm(ks[21], (DEPTH, D_MODEL, FFN_HIDDEN), D_MODEL ** -0.5),
        "ffn_w_down": nrm(ks[22], (DEPTH, FFN_HIDDEN, D_MODEL), FFN_HIDDEN ** -0.5),
        "final_norm_w": 1.0 + nrm(ks[23], (D_MODEL,), 0.02),
    }


def reference(x_prompt, x_sample, state_ssd, cache_conv, state_hgrn,
              ssd_norm_w, ssd_in_w, ssd_conv_w, ssd_conv_b, ssd_dt_bias, ssd_A_log, ssd_D,
              ssd_gnorm_w, ssd_out_w, hgrn_norm_w, hgrn_in_w, hgrn_lower_bounds, hgrn_gnorm_w,
              hgrn_out_w, ffn_norm_w, ffn_w_gate, ffn_w_up, ffn_w_down, final_norm_w):
    lb_soft = jax.nn.softmax(hgrn_lower_bounds.astype(jnp.float32), axis=0)
    lbs = jnp.cumsum(lb_soft, axis=0) - lb_soft[0]

    def trunk(x, ssd_states, conv_bufs, hgrn_states):
        new_ssd, new_conv, new_hgrn = [], [], []
        for layer in range(DEPTH):
            j = layer // 2
            h = rmsnorm(x, ssd_norm_w[j] if layer % 2 == 0 else hgrn_norm_w[j])
            if layer % 2 == 0:
                out, cb, ss = ssd_mixer(h, conv_bufs[j], ssd_states[j], ssd_in_w[j], ssd_conv_w[j],
                                        ssd_conv_b[j], ssd_dt_bias[j], ssd_A_log[j], ssd_D[j],
                                        ssd_gnorm_w[j], ssd_out_w[j])
                new_conv.append(cb)
                new_ssd.append(ss)
            else:
                out, hs = hgrn_mixer(h, hgrn_states[j], lbs[layer], hgrn_in_w[j], hgrn_gnorm_w[j], hgrn_out_w[j])
                new_hgrn.append(hs)
            x = x + out.astype(x.dtype)
            x = x + swiglu(rmsnorm(x, ffn_norm_w[layer]), ffn_w_gate[layer], ffn_w_up[layer], ffn_w_down[layer])
        y = rmsnorm(x, final_norm_w)
        return y, jnp.stack(new_ssd), jnp.stack(new_conv), jnp.stack(new_hgrn)

    bp = x_prompt.shape[0]
    zero_ssd = jnp.zeros((N_SSD_LAYERS, bp, SSD_N_HEADS, SSD_HEAD_DIM, SSD_D_STATE), jnp.float32)
    zero_conv = jnp.zeros((N_SSD_LAYERS, bp, SSD_CONV_W - 1, SSD_CONV_DIM), x_prompt.dtype)
    zero_hgrn = jnp.zeros((N_HGRN_LAYERS, bp, HGRN_N_HEADS, HGRN_HEAD_DIM, HGRN_HEAD_DIM), jnp.float32)
    y_prompt, ssd_p, conv_p, hgrn_p = trunk(x_prompt, zero_ssd, zero_conv, zero_hgrn)
    y_sample, ssd_s, conv_s, hgrn_s = trunk(x_sample, state_ssd, cache_conv, state_hgrn)
    return (y_prompt, y_sample, ssd_p, conv_p, hgrn_p, ssd_s, conv_s, hgrn_s)
```

```python
import contextlib
import numpy as np
import concourse.bass as bass
import concourse.mybir as mybir
from concourse.bass_utils import run_bass_kernel_spmd

F32 = mybir.dt.float32
BF16 = mybir.dt.bfloat16
AF = mybir.ActivationFunctionType
OP = mybir.AluOpType
AX = mybir.AxisListType

D = 1024
DI = 2048
NH = 32
HP = 64
NG = 4
NS = 128
CONV = 3072
INW = 5152
FF = 2816
NFB = FF // 128
EPS = 1e-6
SEQ = 8192
NSAMP = 2
LS = 32
SEM_LIMIT = 2000


class Buf:
    __slots__ = ("name", "w", "r")

    def __init__(self, name):
        self.name = name
        self.w = None
        self.r = []


class Sched:
    ENG = ("pe", "act", "dve", "pool", "sp")

    def __init__(self, nc, stack):
        self.nc = nc
        self.stack = stack
        self.q = {k: [] for k in self.ENG}
        self.sems = {}
        self.cur = {}
        self.cnt = {}
        self.waited = {k: {} for k in self.ENG}
        self.pending = {k: False for k in self.ENG}
        self.last_tok = {}
        self.nsem = 0
        for k in self.ENG:
            self._new_sem(k)
        self.dsem = {}
        self.dcnt = {}
        self.drot = {}

    def _mk(self, name):
        self.nsem += 1
        return self.stack.enter_context(self.nc.semaphore(name))

    def _new_sem(self, k):
        idx = self.cur.get(k, (None, -1))[1] + 1
        name = f"s_{k}_{idx}"
        self.sems[name] = self._mk(name)
        self.cur[k] = (name, idx)
        self.cnt[k] = 0

    def _wait(self, eng, deps):
        best = {}
        for (s, v) in deps:
            if v > best.get(s, 0):
                best[s] = v
        for s, v in best.items():
            if self.waited[eng].get(s, 0) >= v:
                continue
            self.waited[eng][s] = v
            sem = self.sems[s]
            self.q[eng].append(lambda e, sem=sem, v=v: e.wait_ge(sem, v))

    def _collect(self, eng, reads, writes):
        deps = []
        own = self.cur[eng][0]
        for b in reads:
            if b.w is not None:
                deps.append(b.w)
        for b in writes:
            if b.w is not None:
                deps.append(b.w)
            deps.extend(b.r)
        if eng == "pe":
            deps = [d for d in deps if d[0] != own]
        return deps

    def op(self, eng, fn, reads=(), writes=(), signal=True):
        if self.cnt[eng] >= SEM_LIMIT and not self.pending[eng]:
            self._new_sem(eng)
        self._wait(eng, self._collect(eng, reads, writes))
        if signal:
            self.cnt[eng] += 1
            name = self.cur[eng][0]
            sem = self.sems[name]
            tok = (name, self.cnt[eng])
            self.last_tok[eng] = tok
            self.pending[eng] = False
            self.q[eng].append(lambda e, fn=fn, sem=sem: fn(e).then_inc(sem, 1))
        else:
            tok = (self.cur[eng][0], self.cnt[eng] + 1)
            self.pending[eng] = True
            self.q[eng].append(lambda e, fn=fn: fn(e))
        for b in reads:
            b.r.append(tok)
        for b in writes:
            b.w = tok
            b.r = []
        return tok

    def dma(self, eng, key, out, in_, reads=(), writes=(), **kw):
        if key not in self.dsem or self.dcnt[key] >= SEM_LIMIT:
            self.drot[key] = self.drot.get(key, -1) + 1
            name = f"d_{key}_{self.drot[key]}"
            self.sems[name] = self._mk(name)
            self.dsem[key] = name
            self.dcnt[key] = 0
        self._wait(eng, self._collect(eng, reads, writes))
        self.dcnt[key] += 16
        name = self.dsem[key]
        sem = self.sems[name]
        tok = (name, self.dcnt[key])
        self.q[eng].append(lambda e, sem=sem: e.dma_start(out=out, in_=in_, **kw).then_inc(sem, 16))
        for b in reads:
            b.r.append(tok)
        for b in writes:
            b.w = tok
            b.r = []
        return tok

    def barrier(self):
        toks = list(self.last_tok.values())
        for key, name in self.dsem.items():
            if self.dcnt[key] > 0:
                toks.append((name, self.dcnt[key]))
        for k in self.ENG:
            self._wait(k, toks)

    def finish(self):
        self.barrier()


class Arena:
    def __init__(self, ap_f32, ncols):
        self.ap = ap_f32
        self.ncols = ncols
        self.off = 0
        self.mark = 0

    def alloc(self, cols, dtype=F32, parts=128):
        w = cols if dtype == F32 else (cols + 1) // 2
        assert self.off + w <= self.ncols, f"arena overflow {self.off}+{w}>{self.ncols}"
        a = self.ap[:, self.off:self.off + w]
        self.off += w
        if dtype != F32:
            a = a.bitcast(dtype)[:, 0:cols]
        return a

    def set_mark(self):
        self.mark = self.off

    def reset(self):
        self.off = self.mark


def build(Tp):
    NT = Tp // 128
    TT = Tp + NSAMP * LS
    nc = bass.Bass("TRN2", target_bir_lowering=False)

    def din(name, shape):
        return nc.dram_tensor(name, list(shape), F32, kind="ExternalInput").ap()

    def dout(name, shape):
        return nc.dram_tensor(name, list(shape), F32, kind="ExternalOutput").ap()

    xp = din("xp", [Tp, D])
    xsm = din("xsm", [NSAMP * LS, D])
    st_ssd = din("st_ssd", [NSAMP, DI, NS])
    st_conv = din("st_conv", [NSAMP, 128, 72])
    st_hgrn = din("st_hgrn", [NSAMP, 8, 128, 128])
    consts = din("consts", [128, 4 * 128])
    w_in = din("w_in", [D, INW])
    w_out = din("w_out", [DI, D])
    h_in = din("h_in", [D, 4 * D])
    h_out = din("h_out", [D, D])
    f_gate = din("f_gate", [2, D, FF])
    f_up = din("f_up", [2, D, FF])
    f_down = din("f_down", [2, FF, D])
    nw_all = din("nw_all", [128, 4 * 8])
    convw = din("convw", [128, 5 * 24])
    convb_row = din("convb_row", [1, CONV])
    headp = din("headp", [1, 96])
    gnw = din("gnw", [128, 16])
    lbT = din("lbT", [128, 16])
    hgn = din("hgn", [1, 128])
    fnw = din("fnw", [1, D])

    y_p = dout("y_p", [Tp, D])
    y_s = dout("y_s", [NSAMP * LS, D])
    o_ssd = dout("o_ssd", [1 + NSAMP, DI, NS])
    o_conv = dout("o_conv", [1 + NSAMP, 128, 72])
    o_hgrn = dout("o_hgrn", [1 + NSAMP, 8, 128, 128])

    ygn_d = nc.dram_tensor("ygn_d", [TT, DI], BF16).ap()
    xa = nc.dram_tensor("xa", [TT, D], F32).ap()
    xb = nc.dram_tensor("xb", [TT, D], F32).ap()
    xc = nc.dram_tensor("xc", [TT, D], F32).ap()

    seqs = [(0, Tp, 0)] + [(Tp + i * LS, LS, 1 + i) for i in range(NSAMP)]

    with contextlib.ExitStack() as stack:
        ARENA_COLS = 53000
        arena_t = stack.enter_context(nc.sbuf_tensor("arena", [128, ARENA_COLS], F32))
        psum_t = stack.enter_context(nc.psum_tensor("psum", [128, 4096], F32))
        S = Sched(nc, stack)
        A = Arena(arena_t[:, :], ARENA_COLS)
        PS = psum_t[:, :]
        pb = [Buf(f"ps{i}") for i in range(8)]

        def bank(i, n=128, lo=0, hi=512):
            return PS[:n, i * 512 + lo:i * 512 + hi]

        def bank_bf(i, n=128):
            return PS[:n, i * 512:(i + 1) * 512].bitcast(BF16)

        cst = A.alloc(512)
        ident = cst[:, 0:128]
        triu = cst[:, 128:256]
        strictl = cst[:, 256:384]
        ones = cst[:, 384:512]
        ident_bf = A.alloc(128, BF16)
        nwt = A.alloc(32)
        cw = A.alloc(120)
        hp_bc = A.alloc(96)
        gnw_t = A.alloc(16)
        lb_t = A.alloc(16)
        oml_t = A.alloc(8)
        lb1_t = A.alloc(8)
        hgn_bc = A.alloc(128)
        eps_t = A.alloc(1)
        one_t = A.alloc(1)
        mhalf = A.alloc(8)
        b_const = Buf("const")

        S.dma("sp", "c", cst, consts, writes=[b_const])
        S.dma("sp", "c", nwt, nw_all, writes=[b_const])
        S.dma("sp", "c", cw, convw, writes=[b_const])
        S.dma("sp", "c", hp_bc, headp.broadcast_to([128, 96]), writes=[b_const])
        S.dma("sp", "c", gnw_t, gnw, writes=[b_const])
        S.dma("sp", "c", lb_t, lbT, writes=[b_const])
        S.dma("sp", "c", hgn_bc, hgn.broadcast_to([128, 128]), writes=[b_const])
        S.op("dve", lambda e: e.tensor_copy(out=ident_bf, in_=ident), reads=[b_const], writes=[b_const])
        S.op("dve", lambda e: e.memset(eps_t, EPS), writes=[b_const])
        S.op("dve", lambda e: e.memset(one_t, 1.0), writes=[b_const])
        S.op("dve", lambda e: e.memset(mhalf, -0.5), writes=[b_const])
        S.op("act", lambda e: e.activation(out=hp_bc[:, 32:64], in_=hp_bc[:, 32:64], func=AF.Exp), reads=[b_const], writes=[b_const])
        S.op("dve", lambda e: e.tensor_scalar(out=hp_bc[:, 32:64], in0=hp_bc[:, 32:64], scalar1=-1.0, scalar2=None, op0=OP.mult),
             reads=[b_const], writes=[b_const])
        S.op("act", lambda e: e.activation(out=lb_t, in_=lb_t, func=AF.Exp), reads=[b_const], writes=[b_const])
        S.op("dve", lambda e: e.tensor_tensor(out=oml_t, in0=lb_t[:, 0:8], in1=lb_t[:, 8:16], op=OP.add), reads=[b_const], writes=[b_const])
        S.op("dve", lambda e: e.reciprocal(out=oml_t, in_=oml_t), reads=[b_const], writes=[b_const])
        S.op("dve", lambda e: e.tensor_tensor(out=lb1_t, in0=lb_t[:, 8:16], in1=oml_t, op=OP.mult), reads=[b_const], writes=[b_const])
        S.op("dve", lambda e: e.tensor_scalar(out=oml_t, in0=lb1_t, scalar1=-1.0, scalar2=1.0, op0=OP.mult, op1=OP.add),
             reads=[b_const], writes=[b_const])
        dtb_bc = hp_bc[:, 0:32]
        A_bc = hp_bc[:, 32:64]
        D_bc = hp_bc[:, 64:96]
        A.set_mark()

        def interleave(gens, weights=None):
            gens = list(gens)
            weights = list(weights) if weights else [1] * len(gens)
            live = list(range(len(gens)))
            while live:
                for i in list(live):
                    for _ in range(weights[i]):
                        try:
                            next(gens[i])
                        except StopIteration:
                            live.remove(i)
                            break

        def load_weight(dst, src2d, K, N, bw):
            kt = K // 128
            srcv = src2d.rearrange("(k p) n -> p k n", p=128)
            step = max(1, kt // 4)
            for k0 in range(0, kt, step):
                k1 = min(kt, k0 + step)
                S.dma("pool", "w", dst[:, k0:k1, :], srcv[:, k0:k1, :], writes=[bw])

        def rstd_op(dst, src, sc, n, k, b_st):
            S.op("dve", lambda e: e.tensor_scalar(out=dst, in0=src, scalar1=sc, scalar2=EPS, op0=OP.mult, op1=OP.add),
                 reads=[b_st], writes=[b_st])
            S.op("pool", lambda e: e.tensor_tensor(out=dst, in0=dst, in1=mhalf[:n, 0:k], op=OP.pow),
                 reads=[b_st, b_const], writes=[b_st])

        def rmsnorm_hT(n, xt, b_xt, nw_cols, xn, b_xn, hT, b_hT, stat, b_stat, tbank=7):
            ss = stat[:n, 0:1]
            sd = stat[:n, 1:2]
            S.op("act", lambda e: e.activation(out=xn[:n, :], in_=xt[:n, :], func=AF.Square, accum_out=ss),
                 reads=[b_xt], writes=[b_xn, b_stat])
            rstd_op(sd, ss, 1.0 / D, n, 1, b_stat)
            S.op("dve", lambda e: e.tensor_scalar(out=xn[:n, :], in0=xt[:n, :], scalar1=sd, scalar2=None, op0=OP.mult),
                 reads=[b_xt, b_stat], writes=[b_xn])
            pT = bank_bf(tbank)
            for k in range(8):
                S.op("pe", lambda e, k=k: e.transpose(out=pT[:, k * 128:k * 128 + n], in_=xn[:n, k * 128:(k + 1) * 128],
                                                      identity=ident_bf[:n, :n]),
                     reads=[b_xn, b_const], writes=[pb[tbank]], signal=(k == 7))
            S.op("dve", lambda e: e.tensor_tensor(out=hT[:, :, :n],
                                                  in0=pT.rearrange("p (k t) -> p k t", t=128)[:, :, :n],
                                                  in1=nw_cols.unsqueeze(2).broadcast_to([128, 8, n]), op=OP.mult),
                 reads=[pb[tbank], b_const], writes=[b_hT])

        def stage_ssd():
            A.reset()
            Win = A.alloc(8 * INW, BF16).rearrange("p (k n) -> p k n", n=INW)
            bW = Buf("W")
            load_weight(Win, w_in, D, INW, bW)
            xt = A.alloc(D); b_xt = Buf("xt")
            xn = A.alloc(D, BF16); b_xn = Buf("xn")
            hT = A.alloc(8 * 128, BF16).rearrange("p (k t) -> p k t", t=128); b_hT = Buf("hT")
            statA = A.alloc(4); b_statA = Buf("statA")
            raw = A.alloc(24 * 132, BF16).rearrange("p (c t) -> p c t", t=132); b_raw = Buf("raw")
            hist32 = A.alloc(72).rearrange("p (c j) -> p c j", j=3); b_hist = Buf("hist32")
            tt = [A.alloc(512) for _ in range(2)]; b_tt = [Buf("tt0"), Buf("tt1")]
            dg = A.alloc(96 * 128, BF16).rearrange("p (m c) -> p m c", c=128); b_dg = Buf("dg")
            cbrow = A.alloc(CONV, BF16)
            ones_bf = A.alloc(128, BF16)
            for m in range(96):
                S.op("dve", lambda e, m=m: e.tensor_scalar(out=dg[:, m, :], in0=ident, scalar1=cw[:, m:m + 1], scalar2=0.5,
                                                           op0=OP.mult, op1=OP.mult), reads=[b_const], writes=[b_dg])
            S.dma("pool", "cbr", cbrow[0:1, :], convb_row, writes=[b_dg])
            S.op("pool", lambda e: e.tensor_scalar(out=cbrow[0:1, :], in0=cbrow[0:1, :], scalar1=0.5, scalar2=None, op0=OP.mult),
                 reads=[b_dg], writes=[b_dg])
            S.op("pool", lambda e: e.memset(ones_bf[0:1, :], 1.0), writes=[b_dg])
            S.op("pool", lambda e: e.tensor_scalar(out=Win[:, :, 0:2048], in0=Win[:, :, 0:2048], scalar1=0.5, scalar2=None, op0=OP.mult),
                 reads=[bW], writes=[bW])
            xcT = A.alloc(16 * 128, BF16).rearrange("p (c t) -> p c t", t=128); b_xcT = Buf("xcT")
            zs = [A.alloc(DI, BF16) for _ in range(2)]; b_zs = [Buf("zs0"), Buf("zs1")]
            dtv = [A.alloc(32 * 8) for _ in range(2)]; b_dt = [Buf("dt0"), Buf("dt1")]
            etot = [A.alloc(32) for _ in range(2)]; b_etot = [Buf("et0"), Buf("et1")]
            BCT = [A.alloc(8 * 128, BF16).rearrange("p (c t) -> p c t", t=128) for _ in range(2)]; b_BCT = [Buf("BCT0"), Buf("BCT1")]
            xs_tok = [A.alloc(DI, BF16) for _ in range(2)]; b_xs = [Buf("xs0"), Buf("xs1")]
            Btok = [A.alloc(4 * 128, BF16).rearrange("p (g n) -> p g n", n=128) for _ in range(2)]; b_Btok = [Buf("Bt0"), Buf("Bt1")]
            cbm = [A.alloc(4 * 128) for _ in range(2)]; b_cbm = [Buf("cbm0"), Buf("cbm1")]
            Rseg = [A.alloc(512) for _ in range(2)]; b_Rseg = [Buf("R0"), Buf("R1")]
            Eb = [A.alloc(512) for _ in range(2)]; b_E = [Buf("E0"), Buf("E1")]
            Mc = [A.alloc(4 * 128, BF16) for _ in range(2)]; b_Mc = [Buf("M0"), Buf("M1")]
            xdt = [A.alloc(512, BF16) for _ in range(2)]; b_xdt = [Buf("xdt0"), Buf("xdt1")]
            xdtw = [A.alloc(512, BF16) for _ in range(2)]; b_xdtw = [Buf("xdtw0"), Buf("xdtw1")]
            yb = [A.alloc(512) for _ in range(2)]; b_y = [Buf("y0"), Buf("y1")]
            dxs = [A.alloc(512, BF16) for _ in range(2)]; b_dxs = [Buf("dxs0"), Buf("dxs1")]
            ygn_all = A.alloc(2 * DI, BF16)
            ygn = [ygn_all[:, i * DI:(i + 1) * DI] for i in range(2)]; b_ygn = [Buf("ygn0"), Buf("ygn1")]
            sqj = A.alloc(512, BF16); b_sqj = Buf("sqj")
            statB = A.alloc(8); b_statB = Buf("statB")
            St = A.alloc(DI); b_S = [Buf(f"S{g}") for g in range(4)]
            Sbf = A.alloc(DI, BF16); b_Sbf = [Buf(f"Sbf{g}") for g in range(4)]
            tmpS = ygn_all.bitcast(F32)

            def phaseA(n, x_src, sl, last=False):
                dv = dtv[sl]
                dtr, tA, tB, dt_, a_, cum_sb, ecum, wend = [dv[:, i * 32:(i + 1) * 32] for i in range(8)]
                bd = b_dt[sl]
                S.dma("sp", "xt", xt[:n, :], x_src, writes=[b_xt])
                rmsnorm_hT(n, xt, b_xt, nwt[:, 0:8], xn, b_xn, hT, b_hT, statA, b_statA, tbank=2)
                yield
                for q in range(6):
                    bk = q % 2
                    for c in range(4):
                        cbk = q * 4 + c
                        for k in range(8):
                            S.op("pe", lambda e, c=c, cbk=cbk, k=k, bk=bk: e.matmul(
                                bank(bk, 128, c * 128, c * 128 + n), lhsT=Win[:, k, 2048 + cbk * 128:2048 + (cbk + 1) * 128],
                                rhs=hT[:, k, :n], start=(k == 0), stop=(k == 7)),
                                reads=[b_hT, bW], writes=[pb[bk]], signal=(k == 7 and c == 3))
                    yield
                    S.op("act", lambda e, q=q, bk=bk: e.activation(
                        out=raw[:, q * 4:(q + 1) * 4, 3:3 + n],
                        in_=bank(bk).rearrange("p (c t) -> p c t", t=128)[:, :, :n], func=AF.Copy),
                        reads=[pb[bk]], writes=[b_raw])
                    if last:
                        S.op("dve", lambda e, q=q, bk=bk: e.tensor_copy(
                            out=hist32[:, q * 4:(q + 1) * 4, :], in_=bank(bk).rearrange("p (c t) -> p c t", t=128)[:, :, n - 3:n]),
                            reads=[pb[bk]], writes=[b_hist])
                yield
                for k in range(8):
                    S.op("pe", lambda e, k=k: e.matmul(bank(2, n, 0, 32), lhsT=hT[:, k, :n], rhs=Win[:, k, 5120:5152],
                                                       start=(k == 0), stop=(k == 7)),
                         reads=[b_hT, bW], writes=[pb[2]], signal=(k == 7))
                yield
                for j in range(4):
                    bk = j % 2
                    for k in range(8):
                        S.op("pe", lambda e, j=j, k=k, bk=bk: e.matmul(bank(bk, n), lhsT=hT[:, k, :n], rhs=Win[:, k, j * 512:(j + 1) * 512],
                                                                       start=(k == 0), stop=(k == 7)),
                             reads=[b_hT, bW], writes=[pb[bk]], signal=(k == 7))
                    yield
                    ti = j % 2
                    S.op("act", lambda e, bk=bk, ti=ti: e.activation(out=tt[ti][:n, :], in_=bank(bk, n), func=AF.Tanh),
                         reads=[pb[bk]], writes=[b_tt[ti]])
                    S.op("dve", lambda e, j=j, bk=bk, ti=ti: e.scalar_tensor_tensor(out=zs[sl][:n, j * 512:(j + 1) * 512], in0=tt[ti][:n, :], scalar=1.0,
                                                                                  in1=bank(bk, n), op0=OP.add, op1=OP.mult),
                         reads=[pb[bk], b_tt[ti]], writes=[b_zs[sl]])
                yield
                S.op("dve", lambda e: e.tensor_tensor(out=dtr[:n], in0=bank(2, n, 0, 32), in1=dtb_bc[:n], op=OP.add),
                     reads=[pb[2], b_const], writes=[bd])
                S.op("dve", lambda e: e.tensor_scalar_max(out=tB[:n], in0=dtr[:n], scalar1=0.0), reads=[bd], writes=[bd])
                S.op("dve", lambda e: e.scalar_tensor_tensor(out=tA[:n], in0=tB[:n], scalar=-2.0, in1=dtr[:n], op0=OP.mult, op1=OP.add),
                     reads=[bd], writes=[bd])
                S.op("act", lambda e: e.activation(out=tA[:n], in_=tA[:n], func=AF.Exp), reads=[bd], writes=[bd])
                S.op("act", lambda e: e.activation(out=tA[:n], in_=tA[:n], func=AF.Ln, bias=one_t[:n, :], scale=1.0),
                     reads=[bd, b_const], writes=[bd])
                S.op("dve", lambda e: e.tensor_tensor(out=dt_[:n], in0=tA[:n], in1=tB[:n], op=OP.add), reads=[bd], writes=[bd])
                S.op("dve", lambda e: e.tensor_tensor(out=a_[:n], in0=dt_[:n], in1=A_bc[:n], op=OP.mult), reads=[bd, b_const], writes=[bd])
                yield
                S.op("pe", lambda e: e.matmul(bank(2, n, 32, 64), lhsT=triu[:n, :n], rhs=a_[:n], start=True, stop=True),
                     reads=[bd, b_const], writes=[pb[2]])
                S.op("pe", lambda e: e.matmul(bank(2, 128, 64, 96), lhsT=ones[:n, :128], rhs=a_[:n], start=True, stop=True),
                     reads=[bd, b_const], writes=[pb[2]])
                S.op("act", lambda e: e.activation(out=ecum[:n], in_=bank(2, n, 32, 64), func=AF.Exp), reads=[pb[2]], writes=[bd])
                S.op("act", lambda e: e.activation(out=etot[sl], in_=bank(2, 128, 64, 96), func=AF.Exp), reads=[pb[2]], writes=[b_etot[sl]])
                S.op("dve", lambda e: e.tensor_copy(out=cum_sb[:n], in_=bank(2, n, 32, 64)), reads=[pb[2]], writes=[bd])
                S.op("dve", lambda e: e.tensor_tensor(out=wend[:n], in0=bank(2, n, 64, 96), in1=cum_sb[:n], op=OP.subtract),
                     reads=[pb[2], bd], writes=[bd])
                S.op("act", lambda e: e.activation(out=wend[:n], in_=wend[:n], func=AF.Exp), reads=[bd], writes=[bd])
                S.op("dve", lambda e: e.tensor_tensor(out=tA[:n], in0=dt_[:n], in1=wend[:n], op=OP.mult), reads=[bd], writes=[bd])
                yield
                for q in range(6):
                    yield
                    bk = q % 2
                    for c in range(4):
                        cbk = q * 4 + c
                        for tap in range(4):
                            S.op("pe", lambda e, c=c, cbk=cbk, tap=tap, bk=bk: e.matmul(
                                bank(bk, 128, c * 128, c * 128 + n), lhsT=dg[:, tap * 24 + cbk, :], rhs=raw[:, cbk, tap:tap + n],
                                start=(tap == 0), stop=False), reads=[b_raw, b_dg], writes=[pb[bk]], signal=False)
                        S.op("pe", lambda e, c=c, cbk=cbk, bk=bk: e.matmul(
                            bank(bk, 128, c * 128, c * 128 + n), lhsT=cbrow[0:1, cbk * 128:(cbk + 1) * 128], rhs=ones_bf[0:1, :n],
                            start=False, stop=True), reads=[b_dg], writes=[pb[bk]], signal=(c == 3))
                    ti = q % 2
                    pv = bank(bk).rearrange("p (c t) -> p c t", t=128)[:, :, :n]
                    tv = tt[ti].rearrange("p (c t) -> p c t", t=128)[:, :, :n]
                    S.op("act", lambda e, pv=pv, tv=tv: e.activation(out=tv, in_=pv, func=AF.Tanh), reads=[pb[bk]], writes=[b_tt[ti]])
                    if q < 4:
                        S.op("dve", lambda e, q=q, pv=pv, tv=tv: e.scalar_tensor_tensor(out=xcT[:, q * 4:(q + 1) * 4, :n], in0=tv, scalar=1.0, in1=pv,
                                                                                     op0=OP.add, op1=OP.mult),
                             reads=[pb[bk], b_tt[ti]], writes=[b_xcT])
                    else:
                        S.op("dve", lambda e, q=q, pv=pv, tv=tv: e.scalar_tensor_tensor(out=BCT[sl][:, (q - 4) * 4:(q - 3) * 4, :n], in0=tv, scalar=1.0, in1=pv,
                                                                                     op0=OP.add, op1=OP.mult),
                             reads=[pb[bk], b_tt[ti]], writes=[b_BCT[sl]])
                yield
                S.op("pool", lambda e: e.tensor_copy(out=raw[:, :, 0:3], in_=raw[:, :, n:n + 3]), reads=[b_raw], writes=[b_raw])
                for half in range(2):
                    yield
                    pT = bank_bf(2, n)
                    for c in range(8):
                        S.op("pe", lambda e, c=c, half=half, pT=pT: e.transpose(out=pT[:n, c * 128:(c + 1) * 128], in_=xcT[:, half * 8 + c, :n],
                                                                                  identity=ident_bf[:, :]),
                             reads=[b_xcT, b_const], writes=[pb[2]], signal=(c == 7))
                    S.op("act", lambda e, half=half, pT=pT: e.activation(out=xs_tok[sl][:n, half * 1024:(half + 1) * 1024], in_=pT[:n, :], func=AF.Copy),
                         reads=[pb[2]], writes=[b_xs[sl]])
                yield
                pT = bank_bf(2, n)
                for g in range(4):
                    S.op("pe", lambda e, g=g, pT=pT: e.transpose(out=pT[:n, g * 128:(g + 1) * 128], in_=BCT[sl][:, g, :n], identity=ident_bf[:, :]),
                         reads=[b_BCT[sl], b_const], writes=[pb[2]], signal=(g == 3))
                S.op("act", lambda e, pT=pT: e.activation(out=Btok[sl][:n, :, :], in_=pT[:n, 0:512].rearrange("p (g n) -> p g n", n=128), func=AF.Copy),
                     reads=[pb[2]], writes=[b_Btok[sl]])
                yield
                for g in range(4):
                    S.op("pe", lambda e, g=g: e.matmul(bank(2, n, g * 128, g * 128 + n), lhsT=BCT[sl][:, g, :n], rhs=BCT[sl][:, 4 + g, :n],
                                                       start=True, stop=True),
                         reads=[b_BCT[sl]], writes=[pb[2]], signal=(g == 3))
                cbv = cbm[sl][:n, :].rearrange("p (g t) -> p g t", t=128)
                S.op("dve", lambda e: e.tensor_tensor(out=cbv[:, :, :n], in0=bank(2, n).rearrange("p (g t) -> p g t", t=128)[:, :, :n],
                                                      in1=triu[:n, :n].unsqueeze(1).broadcast_to([n, 4, n]), op=OP.mult),
                     reads=[pb[2], b_const], writes=[b_cbm[sl]])

            def phaseB(n, y_dst, sl, ysl):
                dv = dtv[sl]
                dtr, tA, tB, dt_, a_, cum_sb, ecum, wend = [dv[:, i * 32:(i + 1) * 32] for i in range(8)]
                bd = b_dt[sl]
                cbv = cbm[sl][:n, :].rearrange("p (g t) -> p g t", t=128)
                xs3 = xs_tok[sl][:n, :].rearrange("p (h q) -> p h q", q=64)

                def seg_stage(hq):
                    ri = hq % 2
                    bk = 3 + ri
                    Rv = Rseg[ri][:n, :].rearrange("p (h t) -> p h t", t=128)
                    S.op("pool", lambda e, hq=hq, Rv=Rv: e.tensor_tensor(
                        out=Rv[:, :, :n], in0=a_[:n, hq * 4:(hq + 1) * 4].unsqueeze(2).broadcast_to([n, 4, n]),
                        in1=triu[:n, :n].unsqueeze(1).broadcast_to([n, 4, n]), op=OP.mult),
                        reads=[bd, b_const], writes=[b_Rseg[ri]])
                    for hh in range(4):
                        S.op("pe", lambda e, hh=hh, Rv=Rv, bk=bk: e.matmul(bank(bk, n, hh * 128, hh * 128 + n), lhsT=strictl[:n, :n],
                                                                         rhs=Rv[:, hh, :n], start=True, stop=True),
                             reads=[b_Rseg[ri], b_const], writes=[pb[bk]], signal=(hh == 3))

                def ey_stage(hq, g, gs):
                    ri = hq % 2
                    bk = 3 + ri
                    Ev = Eb[ri][:n, :].rearrange("p (h t) -> p h t", t=128)
                    Mv = Mc[ri][:n, :].rearrange("p (h t) -> p h t", t=128)
                    S.op("act", lambda e, Ev=Ev, bk=bk: e.activation(out=Ev[:, :, :n],
                                                                    in_=bank(bk, n).rearrange("p (h t) -> p h t", t=128)[:, :, :n], func=AF.Exp),
                         reads=[pb[bk]], writes=[b_E[ri]])
                    S.op("dve", lambda e, Mv=Mv, Ev=Ev, g=g: e.tensor_tensor(
                        out=Mv[:, :, :n], in0=Ev[:, :, :n],
                        in1=cbv[:, g, :n].unsqueeze(1).broadcast_to([n, 4, n]), op=OP.mult),
                        reads=[b_E[ri], b_cbm[sl]], writes=[b_Mc[ri]])
                    for hh in range(4):
                        hl = (hq % 2) * 4 + hh
                        S.op("pe", lambda e, hl=hl, hh=hh, Mv=Mv, gs=gs: e.matmul(bank(5, n, hl * 64, hl * 64 + 64), lhsT=Mv[:, hh, :n],
                                                                               rhs=xdt[gs][:n, hl * 64:(hl + 1) * 64], start=True, stop=True),
                             reads=[b_Mc[ri], b_xdt[gs]], writes=[pb[5]], signal=(hh == 3))

                seg_stage(0)
                for g in range(4):
                    gs = g % 2
                    S.op("pool", lambda e, g=g, gs=gs: e.tensor_tensor(out=xdt[gs][:n, :].rearrange("p (h q) -> p h q", q=64),
                                                                       in0=xs3[:, g * 8:(g + 1) * 8, :],
                                                                       in1=dt_[:n, g * 8:(g + 1) * 8].unsqueeze(2).broadcast_to([n, 8, 64]), op=OP.mult),
                         reads=[b_xs[sl], bd], writes=[b_xdt[gs]])
                    S.op("pool", lambda e, g=g, gs=gs: e.tensor_tensor(out=xdtw[gs][:n, :].rearrange("p (h q) -> p h q", q=64),
                                                                       in0=xs3[:, g * 8:(g + 1) * 8, :],
                                                                       in1=tA[:n, g * 8:(g + 1) * 8].unsqueeze(2).broadcast_to([n, 8, 64]), op=OP.mult),
                         reads=[b_xs[sl], bd], writes=[b_xdtw[gs]])
                    S.op("pool", lambda e, g=g, gs=gs: e.tensor_tensor(out=dxs[gs][:n, :].rearrange("p (h q) -> p h q", q=64),
                                                                       in0=xs3[:, g * 8:(g + 1) * 8, :],
                                                                       in1=D_bc[:n, g * 8:(g + 1) * 8].unsqueeze(2).broadcast_to([n, 8, 64]), op=OP.mult),
                         reads=[b_xs[sl], b_const], writes=[b_dxs[gs]])
                    yield
                    for hq in (2 * g, 2 * g + 1):
                        if hq + 1 < 8:
                            seg_stage(hq + 1)
                        yield
                        ey_stage(hq, g, gs)
                        yield
                    yield
                    S.op("pe", lambda e, g=g: e.matmul(bank(6, n), lhsT=BCT[sl][:, 4 + g, :n], rhs=Sbf[:, g * 512:(g + 1) * 512],
                                                       start=True, stop=True),
                         reads=[b_BCT[sl], b_Sbf[g]], writes=[pb[6]])
                    yv = yb[gs][:n, :]
                    S.op("dve", lambda e, g=g, yv=yv: e.tensor_tensor(
                        out=yv.rearrange("p (h q) -> p h q", q=64), in0=bank(6, n).rearrange("p (h q) -> p h q", q=64),
                        in1=ecum[:n, g * 8:(g + 1) * 8].unsqueeze(2).broadcast_to([n, 8, 64]), op=OP.mult),
                        reads=[pb[6], bd], writes=[b_y[gs]])
                    S.op("dve", lambda e, yv=yv: e.tensor_tensor(out=yv, in0=yv, in1=bank(5, n), op=OP.add),
                         reads=[pb[5], b_y[gs]], writes=[b_y[gs]])
                    S.op("pool", lambda e, yv=yv, gs=gs: e.tensor_tensor(out=yv, in0=yv, in1=dxs[gs][:n, :], op=OP.add),
                         reads=[b_y[gs], b_dxs[gs]], writes=[b_y[gs]])
                    S.op("pool", lambda e, yv=yv, g=g: e.tensor_tensor(out=yv, in0=yv, in1=zs[sl][:n, g * 512:(g + 1) * 512], op=OP.mult),
                         reads=[b_y[gs], b_zs[sl]], writes=[b_y[gs]])
                    yield
                    S.op("act", lambda e, yv=yv, g=g: e.activation(out=sqj[:n, :], in_=yv, func=AF.Square, accum_out=statB[:n, g:g + 1]),
                         reads=[b_y[gs]], writes=[b_sqj, b_statB])
                    rstd_op(statB[:n, 4 + g:5 + g], statB[:n, g:g + 1], 1.0 / 512, n, 1, b_statB)
                    S.op("dve", lambda e, yv=yv, g=g: e.tensor_scalar(out=ygn[ysl][:n, g * 512:(g + 1) * 512], in0=yv, scalar1=statB[:n, 4 + g:5 + g],
                                                                      scalar2=None, op0=OP.mult),
                         reads=[b_y[gs], b_statB], writes=[b_ygn[ysl]])
                    yield
                    S.op("pe", lambda e, g=g, gs=gs: e.matmul(bank(7), lhsT=Btok[sl][:n, g, :], rhs=xdtw[gs][:n, :], start=True, stop=True),
                         reads=[b_Btok[sl], b_xdtw[gs]], writes=[pb[7]])
                    Sg = St[:, g * 512:(g + 1) * 512]
                    S.op("pool", lambda e, g=g, Sg=Sg: e.tensor_tensor(out=Sg.rearrange("p (h q) -> p h q", q=64), in0=Sg.rearrange("p (h q) -> p h q", q=64),
                                                                       in1=etot[sl][:, g * 8:(g + 1) * 8].unsqueeze(2).broadcast_to([128, 8, 64]), op=OP.mult),
                         reads=[b_S[g], b_etot[sl]], writes=[b_S[g]])
                    S.op("dve", lambda e, Sg=Sg: e.tensor_tensor(out=Sg, in0=Sg, in1=bank(7), op=OP.add), reads=[pb[7], b_S[g]], writes=[b_S[g]])
                    S.op("act", lambda e, g=g, Sg=Sg: e.activation(out=Sbf[:, g * 512:(g + 1) * 512], in_=Sg, func=AF.Copy),
                         reads=[b_S[g]], writes=[b_Sbf[g]])
                S.dma("sp", f"yg{ysl}", y_dst, ygn[ysl][:n, :], reads=[b_ygn[ysl]])

            cnt = 0
            ycnt = 0
            for (off, L, kind) in seqs:
                if kind == 0:
                    S.op("pool", lambda e: e.memset(St, 0.0), writes=b_S)
                    S.op("pool", lambda e: e.memset(Sbf, 0.0), writes=b_Sbf)
                    S.op("pool", lambda e: e.memset(raw[:, :, 0:3], 0.0), writes=[b_raw])
                    src = xp
                    soff = 0
                else:
                    i = kind - 1
                    tmp = tmpS.rearrange("p (b n) -> p b n", n=128)
                    S.dma("sp", "st", tmp, st_ssd[i].rearrange("(b p) n -> p b n", p=128), writes=b_ygn)
                    for g in range(4):
                        for b in range(4):
                            S.op("pe", lambda e, g=g, b=b: e.transpose(out=bank(7, 128, b * 128, b * 128 + 128), in_=tmp[:, g * 4 + b, :], identity=ident),
                                 reads=b_ygn + [b_const], writes=[pb[7]], signal=(b == 3))
                        S.op("dve", lambda e, g=g: e.tensor_copy(out=St[:, g * 512:(g + 1) * 512], in_=bank(7)), reads=[pb[7]], writes=[b_S[g]])
                        S.op("act", lambda e, g=g: e.activation(out=Sbf[:, g * 512:(g + 1) * 512], in_=St[:, g * 512:(g + 1) * 512], func=AF.Copy),
                             reads=[b_S[g]], writes=[b_Sbf[g]])
                    S.dma("pool", "stc", raw[:, :, 0:3], st_conv[i].rearrange("p (c j) -> p c j", j=3), writes=[b_raw])
                    src = xsm
                    soff = i * LS
                nt = max(1, L // 128)
                n = min(L, 128)
                sls = []
                for t in range(nt):
                    sls.append(cnt % 2)
                    cnt += 1
                interleave([phaseA(n, src[soff:soff + n, :], sls[0], last=(nt == 1))])
                for t in range(nt):
                    gens = [phaseB(n, ygn_d[off + t * n:off + (t + 1) * n, :], sls[t], ycnt % 2)]
                    if t + 1 < nt:
                        gens.append(phaseA(n, src[soff + (t + 1) * n:soff + (t + 2) * n, :], sls[t + 1], last=(t + 2 == nt)))
                    interleave(gens, [4, 3])
                    ycnt += 1
                S.dma("sp", "soc", o_conv[kind].rearrange("p (c j) -> p c j", j=3), hist32, reads=[b_hist])
                tmp = tmpS.rearrange("p (b n) -> p b n", n=128)
                for g in range(4):
                    for b in range(4):
                        S.op("pe", lambda e, g=g, b=b: e.transpose(out=bank(7, 128, b * 128, b * 128 + 128),
                                                                   in_=St[:, (g * 4 + b) * 128:(g * 4 + b + 1) * 128], identity=ident),
                             reads=[b_S[g], b_const], writes=[pb[7]], signal=(b == 3))
                    S.op("dve", lambda e, g=g: e.tensor_copy(out=tmpS[:, g * 512:(g + 1) * 512], in_=bank(7)), reads=[pb[7]], writes=b_ygn)
                S.dma("sp", "so", o_ssd[kind].rearrange("(b p) n -> p b n", p=128), tmp, reads=b_ygn)

        ffn0_w = Buf("W_ffn0")

        def stage_oproj():
            A.reset()
            Wg0 = A.alloc(8 * FF, BF16).rearrange("p (k n) -> p k n", n=FF)
            Wu0 = A.alloc(8 * FF, BF16).rearrange("p (k n) -> p k n", n=FF)
            Wd0 = A.alloc(NFB * D, BF16).rearrange("p (k n) -> p k n", n=D)
            Wout = A.alloc(16 * D, BF16).rearrange("p (k n) -> p k n", n=D)
            bW = Buf("W")
            load_weight(Wout, w_out, DI, D, bW)
            load_weight(Wg0, f_gate[0], D, FF, ffn0_w)
            load_weight(Wu0, f_up[0], D, FF, ffn0_w)
            load_weight(Wd0, f_down[0], FF, D, ffn0_w)
            xt = [A.alloc(D) for _ in range(2)]; b_xt = [Buf("xt0"), Buf("xt1")]
            ygt = [A.alloc(DI, BF16) for _ in range(2)]; b_ygt = [Buf("ygt0"), Buf("ygt1")]
            ygT = [A.alloc(16 * 128, BF16).rearrange("p (k t) -> p k t", t=128) for _ in range(2)]; b_ygT = [Buf("ygT0"), Buf("ygT1")]

            def op_tile(n, r0, kind, sl_):
                xtt = xt[sl_]
                bxt = b_xt[sl_]
                yt = ygt[sl_]
                yT = ygT[sl_]
                src = xp[r0:r0 + n, :] if kind == 0 else xsm[(kind - 1) * LS:(kind - 1) * LS + n, :]
                S.dma("sp", f"ox{sl_}", xtt[:n, :], src, writes=[bxt])
                S.dma("sp", f"oy{sl_}", yt[:n, :], ygn_d[r0:r0 + n, :], writes=[b_ygt[sl_]])
                for half in range(2):
                    tb = 4 + 2 * sl_ + half
                    pT = bank_bf(tb)
                    for c in range(8):
                        S.op("pe", lambda e, c=c, half=half, pT=pT: e.transpose(
                            out=pT[:, c * 128:c * 128 + n], in_=yt[:n, (half * 8 + c) * 128:(half * 8 + c + 1) * 128], identity=ident_bf[:n, :n]),
                            reads=[b_ygt[sl_], b_const], writes=[pb[tb]], signal=(c == 7))
                    S.op("dve", lambda e, half=half, pT=pT: e.tensor_tensor(
                        out=yT[:, half * 8:(half + 1) * 8, :n], in0=pT.rearrange("p (k t) -> p k t", t=128)[:, :, :n],
                        in1=gnw_t[:, half * 8:(half + 1) * 8].unsqueeze(2).broadcast_to([128, 8, n]), op=OP.mult),
                        reads=[pb[tb], b_const], writes=[b_ygT[sl_]])
                yield
                for j in range(2):
                    bk = 2 * sl_ + j
                    for k in range(16):
                        S.op("pe", lambda e, j=j, k=k, bk=bk: e.matmul(bank(bk, n), lhsT=yT[:, k, :n], rhs=Wout[:, k, j * 512:(j + 1) * 512],
                                                                       start=(k == 0), stop=(k == 15)),
                             reads=[b_ygT[sl_], bW], writes=[pb[bk]], signal=(k == 15))
                    S.op("dve", lambda e, j=j, bk=bk: e.tensor_tensor(out=xtt[:n, j * 512:(j + 1) * 512], in0=xtt[:n, j * 512:(j + 1) * 512],
                                                                      in1=bank(bk, n), op=OP.add),
                         reads=[pb[bk], bxt], writes=[bxt])
                S.dma("sp", f"oo{sl_}", xa[r0:r0 + n, :], xtt[:n, :], reads=[bxt])

            tiles = []
            for (off, L, kind) in seqs:
                nt = max(1, L // 128)
                n = min(L, 128)
                for t in range(nt):
                    tiles.append((n, off + t * n, kind))
            for i in range(0, len(tiles), 2):
                interleave([op_tile(*tiles[j], j % 2) for j in range(i, min(i + 2, len(tiles)))])

        def stage_ffn(layer, x_src_d, x_dst_d, final):
            A.reset()
            Wg = A.alloc(8 * FF, BF16).rearrange("p (k n) -> p k n", n=FF)
            Wu = A.alloc(8 * FF, BF16).rearrange("p (k n) -> p k n", n=FF)
            Wd = A.alloc(NFB * D, BF16).rearrange("p (k n) -> p k n", n=D)
            if layer == 0:
                bW = ffn0_w
            else:
                bW = Buf("W")
                load_weight(Wg, f_gate[layer], D, FF, bW)
                load_weight(Wu, f_up[layer], D, FF, bW)
                load_weight(Wd, f_down[layer], FF, D, bW)
            GM = 512
            xt = [A.alloc(D) for _ in range(2)]; b_xt = [Buf("xt0"), Buf("xt1")]
            xr = [A.alloc(D) for _ in range(2)]; b_xr = [Buf("xr0"), Buf("xr1")]
            xn = A.alloc(D, BF16); b_xn = Buf("xn")
            hTg = [A.alloc(8 * GM, BF16).rearrange("p (k t) -> p k t", t=GM) for _ in range(2)]; b_hT = [Buf("hT0"), Buf("hT1")]
            stat = A.alloc(8); b_stat = Buf("stat")
            stat2 = A.alloc(8); b_stat2 = Buf("stat2")
            sg = [A.alloc(GM) for _ in range(2)]; b_sg = [Buf("sg0"), Buf("sg1")]
            aT = A.alloc(NFB * GM, BF16).rearrange("p (k t) -> p k t", t=GM); b_aT = Buf("aT")
            if final:
                fnw_bc = A.alloc(D)
                S.dma("sp", "c2", fnw_bc, fnw.broadcast_to([128, D]), writes=[b_const])
            nwc = nwt[:, (8 if layer == 0 else 24):(16 if layer == 0 else 32)]
            cnt = {"x": 0, "r": 0}

            def prep(r0, G, tn, hs):
                ng = G // tn
                for s in range(ng):
                    sl_ = cnt["x"] % 2
                    cnt["x"] += 1
                    S.dma("sp", f"fx{sl_}", xt[sl_][:tn, :], x_src_d[r0 + s * tn:r0 + (s + 1) * tn, :], writes=[b_xt[sl_]])
                    rmsnorm_hT(tn, xt[sl_], b_xt[sl_], nwc, xn, b_xn, hTg[hs][:, :, s * 128:(s + 1) * 128], b_hT[hs], stat, b_stat)
                    yield

            def main(r0, G, tn, kind, hs):
                ng = G // tn
                hv = hTg[hs]
                for fb in range(NFB):
                    bg = (fb % 2) * 2
                    bu = bg + 1
                    for (W_, bk) in ((Wg, bg), (Wu, bu)):
                        for k in range(8):
                            S.op("pe", lambda e, W_=W_, bk=bk, fb=fb, k=k: e.matmul(
                                bank(bk, 128, 0, G), lhsT=W_[:, k, fb * 128:(fb + 1) * 128], rhs=hv[:, k, :G],
                                start=(k == 0), stop=(k == 7)),
                                reads=[b_hT[hs], bW], writes=[pb[bk]], signal=(k == 7))
                    si = fb % 2
                    S.op("act", lambda e, bg=bg, si=si: e.activation(out=sg[si][:, :G], in_=bank(bg, 128, 0, G), func=AF.Silu),
                         reads=[pb[bg]], writes=[b_sg[si]])
                    S.op("dve", lambda e, bu=bu, si=si, fb=fb: e.tensor_tensor(out=aT[:, fb, :G], in0=sg[si][:, :G], in1=bank(bu, 128, 0, G), op=OP.mult),
                         reads=[pb[bu], b_sg[si]], writes=[b_aT])
                    if fb % 3 == 2:
                        yield
                for s in range(ng):
                    yield
                    sl_ = cnt["r"] % 2
                    cnt["r"] += 1
                    rr = r0 + s * tn
                    xrt = xr[sl_]
                    bxr = b_xr[sl_]
                    S.dma("sp", f"fr{sl_}", xrt[:tn, :], x_src_d[rr:rr + tn, :], writes=[bxr])
                    for j in range(2):
                        bk = 4 + j
                        for fb in range(NFB):
                            S.op("pe", lambda e, j=j, fb=fb, bk=bk, s=s: e.matmul(bank(bk, tn), lhsT=aT[:, fb, s * 128:s * 128 + tn],
                                                                                  rhs=Wd[:, fb, j * 512:(j + 1) * 512],
                                                                                  start=(fb == 0), stop=(fb == NFB - 1)),
                                 reads=[b_aT, bW], writes=[pb[bk]], signal=(fb == NFB - 1))
                        S.op("dve", lambda e, j=j, bk=bk, xrt=xrt: e.tensor_tensor(out=xrt[:tn, j * 512:(j + 1) * 512], in0=xrt[:tn, j * 512:(j + 1) * 512],
                                                                                   in1=bank(bk, tn), op=OP.add),
                             reads=[pb[bk], bxr], writes=[bxr])
                    if not final:
                        S.dma("sp", f"fo{sl_}", x_dst_d[rr:rr + tn, :], xrt[:tn, :], reads=[bxr])
                    else:
                        ss = stat2[:tn, 0:1]
                        sd = stat2[:tn, 1:2]
                        S.op("act", lambda e, xrt=xrt, ss=ss: e.activation(out=sg[0].bitcast(BF16)[:tn, 0:D], in_=xrt[:tn, :], func=AF.Square, accum_out=ss),
                             reads=[bxr], writes=[b_sg[0], b_stat2])
                        rstd_op(sd, ss, 1.0 / D, tn, 1, b_stat2)
                        S.op("dve", lambda e, xrt=xrt, sd=sd: e.scalar_tensor_tensor(out=xrt[:tn, :], in0=xrt[:tn, :], scalar=sd, in1=fnw_bc[:tn, :],
                                                                                     op0=OP.mult, op1=OP.mult),
                             reads=[bxr, b_stat2, b_const], writes=[bxr])
                        if kind == 0:
                            dst = y_p[rr:rr + tn, :]
                        else:
                            dst = y_s[rr - Tp:rr - Tp + tn, :]
                        S.dma("sp", f"fo{sl_}", dst, xrt[:tn, :], reads=[bxr])

            groups = []
            for g0 in range(0, Tp, GM):
                G = min(GM, Tp - g0)
                groups.append((g0, G, 128, 0))
            groups.append((Tp, NSAMP * LS, NSAMP * LS, 1))
            interleave([prep(groups[0][0], groups[0][1], groups[0][2], 0)])
            for gi, (r0, G, tn, kind) in enumerate(groups):
                gens = [main(r0, G, tn, kind, gi % 2)]
                if gi + 1 < len(groups):
                    nr0, nG, ntn, _ = groups[gi + 1]
                    gens.insert(0, prep(nr0, nG, ntn, (gi + 1) % 2))
                interleave(gens)

        def stage_hgrn():
            A.reset()
            Wi = A.alloc(8 * 4 * D, BF16).rearrange("p (k n) -> p k n", n=4 * D)
            Wo = A.alloc(8 * D, BF16).rearrange("p (k n) -> p k n", n=D)
            bW = Buf("W")
            load_weight(Wi, h_in, D, 4 * D, bW)
            load_weight(Wo, h_out, D, D, bW)
            S.op("pool", lambda e: e.tensor_scalar(out=Wi[:, :, 0:2 * D], in0=Wi[:, :, 0:2 * D], scalar1=0.5, scalar2=None, op0=OP.mult),
                 reads=[bW], writes=[bW])
            S.op("pool", lambda e: e.tensor_scalar(out=Wi[:, :, 3 * D:4 * D], in0=Wi[:, :, 3 * D:4 * D], scalar1=0.5, scalar2=None, op0=OP.mult),
                 reads=[bW], writes=[bW])
            c1_t = A.alloc(8)
            c0_t = A.alloc(8)
            S.op("dve", lambda e: e.tensor_scalar(out=c1_t, in0=oml_t, scalar1=0.5, scalar2=None, op0=OP.mult), reads=[b_const], writes=[b_const])
            S.op("dve", lambda e: e.tensor_tensor(out=c0_t, in0=lb1_t, in1=c1_t, op=OP.add), reads=[b_const], writes=[b_const])
            resetm = A.alloc(8 * 128)
            resetm32 = A.alloc(8 * 32)
            S.op("dve", lambda e: e.memset(resetm, 1.0), writes=[b_const])
            S.op("dve", lambda e: e.memset(resetm.rearrange("p (k t) -> p k t", t=128)[:, :, 0:1], 0.0), writes=[b_const])
            S.op("dve", lambda e: e.memset(resetm32, 1.0), writes=[b_const])
            S.op("dve", lambda e: e.memset(resetm32.rearrange("p (k t) -> p k t", t=32)[:, :, 0:1], 0.0), writes=[b_const])

            def f3(dt=F32):
                return A.alloc(8 * 128, dt).rearrange("p (k t) -> p k t", t=128)
            xn = A.alloc(D, BF16); b_xn = Buf("xn")
            hT = A.alloc(8 * 128, BF16).rearrange("p (k t) -> p k t", t=128); b_hT = Buf("hT")
            statA = A.alloc(4); b_statA = Buf("statA")
            tt = [A.alloc(512) for _ in range(2)]; b_tt = [Buf("tt0"), Buf("tt1")]
            qs = f3(); b_qs = Buf("qs")
            fg = f3(); b_fg = Buf("fg")
            kk = f3(); b_kk = Buf("kk")
            bb = f3(); b_bb = Buf("bb")
            ex = f3(); b_ex = Buf("ex")
            kdT = f3(BF16); b_kdT = Buf("kdT")
            xt = [A.alloc(D) for _ in range(2)]; b_xt = [Buf("xt0"), Buf("xt1")]
            qeT = [f3(BF16) for _ in range(2)]; b_qeT = [Buf("qeT0"), Buf("qeT1")]
            keT = [f3(BF16) for _ in range(2)]; b_keT = [Buf("keT0"), Buf("keT1")]
            kd_tok = [f3(BF16) for _ in range(2)]; b_kdtok = [Buf("kdt0"), Buf("kdt1")]
            v_tok = [A.alloc(D, BF16) for _ in range(2)]; b_v = [Buf("v0"), Buf("v1")]
            gate = [A.alloc(D) for _ in range(2)]; b_gate = [Buf("g0"), Buf("g1")]
            elast = [A.alloc(8) for _ in range(2)]; b_el = [Buf("el0"), Buf("el1")]
            scm = f3(BF16); b_scm = Buf("scm")
            o_sb = A.alloc(D); b_o = Buf("o")
            sq = A.alloc(D); b_sq = Buf("sq")
            og = A.alloc(D, BF16); b_og = Buf("og")
            ogT = f3(BF16); b_ogT = Buf("ogT")
            statB = A.alloc(16); b_statB = Buf("statB")
            Sh = A.alloc(8 * 128).rearrange("p (h v) -> p h v", v=128); b_S = Buf("S")
            Shb = A.alloc(8 * 128, BF16).rearrange("p (h v) -> p h v", v=128); b_Sb = Buf("Sb")

            def phaseA(n, x_src, sl):
                S.dma("sp", f"hx{sl}", xt[sl][:n, :], x_src, writes=[b_xt[sl]])
                rmsnorm_hT(n, xt[sl], b_xt[sl], nwt[:, 16:24], xn, b_xn, hT, b_hT, statA, b_statA, tbank=2)
                rm = (resetm if n == 128 else resetm32)
                for part in range(2):
                    yield
                    for blk in range(8):
                        bk = blk // 4
                        c = blk % 4
                        for k in range(8):
                            S.op("pe", lambda e, part=part, blk=blk, bk=bk, c=c, k=k: e.matmul(
                                bank(bk, 128, c * 128, c * 128 + n), lhsT=Wi[:, k, part * D + blk * 128:part * D + (blk + 1) * 128],
                                rhs=hT[:, k, :n], start=(k == 0), stop=(k == 7)),
                                reads=[b_hT, bW], writes=[pb[bk]], signal=(k == 7 and c == 3))
                    for hf in range(2):
                        yield
                        pv = bank(hf).rearrange("p (c t) -> p c t", t=128)[:, :, :n]
                        if part == 0:
                            tv = tt[hf].rearrange("p (c t) -> p c t", t=128)[:, :, :n]
                            S.op("act", lambda e, pv=pv, tv=tv: e.activation(out=tv, in_=pv, func=AF.Tanh), reads=[pb[hf]], writes=[b_tt[hf]])
                            S.op("dve", lambda e, hf=hf, pv=pv, tv=tv: e.scalar_tensor_tensor(out=qs[:, hf * 4:(hf + 1) * 4, :n], in0=tv, scalar=1.0, in1=pv,
                                                                                        op0=OP.add, op1=OP.mult),
                                 reads=[pb[hf], b_tt[hf]], writes=[b_qs])
                        else:
                            S.op("act", lambda e, hf=hf, pv=pv: e.activation(out=fg[:, hf * 4:(hf + 1) * 4, :n], in_=pv, func=AF.Tanh),
                                 reads=[pb[hf]], writes=[b_fg])
                for part in (2, 3):
                    yield
                    for j in range(2):
                        bk = j
                        for k in range(8):
                            S.op("pe", lambda e, part=part, j=j, bk=bk, k=k: e.matmul(
                                bank(bk, n), lhsT=hT[:, k, :n], rhs=Wi[:, k, part * D + j * 512:part * D + (j + 1) * 512],
                                start=(k == 0), stop=(k == 7)), reads=[b_hT, bW], writes=[pb[bk]], signal=(k == 7))
                        if part == 2:
                            S.op("act", lambda e, j=j, bk=bk: e.activation(out=v_tok[sl][:n, j * 512:(j + 1) * 512], in_=bank(bk, n), func=AF.Copy),
                                 reads=[pb[bk]], writes=[b_v[sl]])
                        else:
                            S.op("act", lambda e, j=j, bk=bk: e.activation(out=tt[j][:n, :], in_=bank(bk, n), func=AF.Tanh),
                                 reads=[pb[bk]], writes=[b_tt[j]])
                            S.op("dve", lambda e, j=j, bk=bk: e.scalar_tensor_tensor(out=gate[sl][:n, j * 512:(j + 1) * 512], in0=tt[j][:n, :], scalar=1.0,
                                                                                   in1=bank(bk, n), op0=OP.add, op1=OP.mult),
                                 reads=[pb[bk], b_tt[j]], writes=[b_gate[sl]])
                yield
                S.op("dve", lambda e: e.tensor_tensor(out=fg[:, :, :n], in0=fg[:, :, :n], in1=c1_t.unsqueeze(2).broadcast_to([128, 8, n]), op=OP.mult),
                     reads=[b_fg, b_const], writes=[b_fg])
                S.op("dve", lambda e: e.tensor_tensor(out=fg[:, :, :n], in0=fg[:, :, :n], in1=c0_t.unsqueeze(2).broadcast_to([128, 8, n]), op=OP.add),
                     reads=[b_fg, b_const], writes=[b_fg])
                S.op("pool", lambda e: e.tensor_scalar(out=kk[:, :, :n], in0=fg[:, :, :n], scalar1=-1.0, scalar2=1.0, op0=OP.mult, op1=OP.add),
                     reads=[b_fg], writes=[b_kk])
                yield
                lgp = ex.rearrange("p k t -> p (k t)")[:, 0:8 * n]
                bbp = bb.rearrange("p k t -> p (k t)")[:, 0:8 * n]
                lg3 = lgp.rearrange("p (k t) -> p k t", t=n)
                bb3 = bbp.rearrange("p (k t) -> p k t", t=n)
                S.op("act", lambda e: e.activation(out=lg3, in_=fg[:, :, :n], func=AF.Ln), reads=[b_fg], writes=[b_ex])
                S.op("dve", lambda e: e.tensor_tensor_scan(out=bbp, data0=rm[:, 0:8 * n], data1=lgp, initial=0.0, op0=OP.mult, op1=OP.add),
                     reads=[b_ex, b_const], writes=[b_bb])
                yield
                S.op("act", lambda e: e.activation(out=lg3, in_=bb3, func=AF.Exp), reads=[b_bb], writes=[b_ex])
                S.op("dve", lambda e: e.tensor_tensor(out=qeT[sl][:, :, :n], in0=qs[:, :, :n], in1=lg3, op=OP.mult),
                     reads=[b_qs, b_ex], writes=[b_qeT[sl]])
                yield
                S.op("act", lambda e: e.activation(out=lg3, in_=bb3, func=AF.Exp, scale=-1.0), reads=[b_bb], writes=[b_ex])
                S.op("dve", lambda e: e.tensor_tensor(out=keT[sl][:, :, :n], in0=kk[:, :, :n], in1=lg3, op=OP.mult),
                     reads=[b_kk, b_ex], writes=[b_keT[sl]])
                yield
                S.op("dve", lambda e: e.tensor_tensor(out=lg3, in0=bb3[:, :, n - 1:n].broadcast_to([128, 8, n]), in1=bb3, op=OP.subtract),
                     reads=[b_bb], writes=[b_ex])
                S.op("act", lambda e: e.activation(out=lg3, in_=lg3, func=AF.Exp), reads=[b_ex], writes=[b_ex])
                S.op("dve", lambda e: e.tensor_tensor(out=kdT[:, :, :n], in0=kk[:, :, :n], in1=lg3, op=OP.mult),
                     reads=[b_kk, b_ex], writes=[b_kdT])
                S.op("act", lambda e: e.activation(out=elast[sl], in_=bb3[:, :, n - 1], func=AF.Exp), reads=[b_bb], writes=[b_el[sl]])
                yield
                pT = bank_bf(2, n)
                for h in range(8):
                    S.op("pe", lambda e, h=h: e.transpose(out=pT[:n, h * 128:(h + 1) * 128], in_=kdT[:, h, :n], identity=ident_bf[:, :]),
                         reads=[b_kdT, b_const], writes=[pb[2]], signal=(h == 7))
                S.op("act", lambda e: e.activation(out=kd_tok[sl][:n, :, :], in_=pT[:n, :].rearrange("p (k t) -> p k t", t=128), func=AF.Copy),
                     reads=[pb[2]], writes=[b_kdtok[sl]])

            def phaseB(n, x_dst, sl):
                for h in range(8):
                    bk = 3 + h // 4
                    c = h % 4
                    S.op("pe", lambda e, h=h, bk=bk, c=c: e.matmul(bank(bk, n, c * 128, c * 128 + n), lhsT=keT[sl][:, h, :n], rhs=qeT[sl][:, h, :n],
                                                                    start=True, stop=True),
                         reads=[b_keT[sl], b_qeT[sl]], writes=[pb[bk]], signal=(c == 3))
                yield
                for hf in range(2):
                    S.op("dve", lambda e, hf=hf: e.tensor_tensor(
                        out=scm[:n, hf * 4:(hf + 1) * 4, :n], in0=bank(3 + hf, n).rearrange("p (c t) -> p c t", t=128)[:, :, :n],
                        in1=triu[:n, :n].unsqueeze(1).broadcast_to([n, 4, n]), op=OP.mult),
                        reads=[pb[3 + hf], b_const], writes=[b_scm])
                yield
                for h in range(8):
                    bk = 5 + h // 4
                    c = h % 4
                    S.op("pe", lambda e, h=h, bk=bk, c=c: e.matmul(bank(bk, n, c * 128, (c + 1) * 128), lhsT=scm[:n, h, :n],
                                                                    rhs=v_tok[sl][:n, h * 128:(h + 1) * 128], start=True, stop=False),
                         reads=[b_scm, b_v[sl]], writes=[pb[bk]], signal=False)
                    S.op("pe", lambda e, h=h, bk=bk, c=c: e.matmul(bank(bk, n, c * 128, (c + 1) * 128), lhsT=qeT[sl][:, h, :n],
                                                                    rhs=Shb[:, h, :], start=False, stop=True),
                         reads=[b_qeT[sl], b_Sb], writes=[pb[bk]], signal=(c == 3))
                yield
                for j in range(2):
                    S.op("act", lambda e, j=j: e.activation(out=o_sb[:n, j * 512:(j + 1) * 512], in_=bank(5 + j, n), func=AF.Copy),
                         reads=[pb[5 + j]], writes=[b_o])
                for h in range(8):
                    bk = 3 + h // 4
                    c = h % 4
                    S.op("pe", lambda e, h=h, bk=bk, c=c: e.matmul(bank(bk, 128, c * 128, (c + 1) * 128), lhsT=kd_tok[sl][:n, h, :],
                                                                    rhs=v_tok[sl][:n, h * 128:(h + 1) * 128], start=True, stop=True),
                         reads=[b_kdtok[sl], b_v[sl]], writes=[pb[bk]], signal=(c == 3))
                yield
                S.op("pool", lambda e: e.tensor_tensor(out=Sh, in0=Sh, in1=elast[sl].unsqueeze(2).broadcast_to([128, 8, 128]), op=OP.mult),
                     reads=[b_S, b_el[sl]], writes=[b_S])
                S.op("pool", lambda e: e.tensor_tensor(out=sq[:n, :], in0=o_sb[:n, :], in1=o_sb[:n, :], op=OP.mult), reads=[b_o], writes=[b_sq])
                yield
                for j in range(2):
                    S.op("dve", lambda e, j=j: e.tensor_tensor(out=Sh[:, j * 4:(j + 1) * 4, :], in0=Sh[:, j * 4:(j + 1) * 4, :],
                                                               in1=bank(3 + j).rearrange("p (c v) -> p c v", v=128), op=OP.add),
                         reads=[pb[3 + j], b_S], writes=[b_S])
                S.op("act", lambda e: e.activation(out=Shb, in_=Sh, func=AF.Copy), reads=[b_S], writes=[b_Sb])
                yield
                S.op("dve", lambda e: e.tensor_reduce(out=statB[:n, 0:8], in_=sq[:n, :].rearrange("p (h v) -> p h v", v=128), axis=AX.X, op=OP.add),
                     reads=[b_sq], writes=[b_statB])
                rstd_op(statB[:n, 8:16], statB[:n, 0:8], 1.0 / 128, n, 8, b_statB)
                yield
                S.op("dve", lambda e: e.tensor_tensor(out=sq[:n, :].rearrange("p (h v) -> p h v", v=128),
                                                      in0=o_sb[:n, :].rearrange("p (h v) -> p h v", v=128),
                                                      in1=statB[:n, 8:16].unsqueeze(2).broadcast_to([n, 8, 128]), op=OP.mult),
                     reads=[b_o, b_statB], writes=[b_sq])
                S.op("pool", lambda e: e.tensor_tensor(out=sq[:n, :].rearrange("p (h v) -> p h v", v=128),
                                                       in0=sq[:n, :].rearrange("p (h v) -> p h v", v=128),
                                                       in1=hgn_bc[:n, :].unsqueeze(1).broadcast_to([n, 8, 128]), op=OP.mult),
                     reads=[b_sq, b_const], writes=[b_sq])
                yield
                S.op("pool", lambda e: e.tensor_tensor(out=og[:n, :], in0=sq[:n, :], in1=gate[sl][:n, :], op=OP.mult),
                     reads=[b_sq, b_gate[sl]], writes=[b_og])
                yield
                pT = bank_bf(7)
                for k in range(8):
                    S.op("pe", lambda e, k=k: e.transpose(out=pT[:, k * 128:k * 128 + n], in_=og[:n, k * 128:(k + 1) * 128], identity=ident_bf[:n, :n]),
                         reads=[b_og, b_const], writes=[pb[7]], signal=(k == 7))
                yield
                S.op("act", lambda e: e.activation(out=ogT[:, :, :n], in_=pT.rearrange("p (k t) -> p k t", t=128)[:, :, :n], func=AF.Copy),
                     reads=[pb[7]], writes=[b_ogT])
                for j in range(2):
                    yield
                    bk = 5 + j
                    for k in range(8):
                        S.op("pe", lambda e, j=j, k=k, bk=bk: e.matmul(bank(bk, n), lhsT=ogT[:, k, :n], rhs=Wo[:, k, j * 512:(j + 1) * 512],
                                                                       start=(k == 0), stop=(k == 7)),
                             reads=[b_ogT, bW], writes=[pb[bk]], signal=(k == 7))
                    S.op("dve", lambda e, j=j, bk=bk: e.tensor_tensor(out=xt[sl][:n, j * 512:(j + 1) * 512], in0=xt[sl][:n, j * 512:(j + 1) * 512],
                                                                      in1=bank(bk, n), op=OP.add),
                         reads=[pb[bk], b_xt[sl]], writes=[b_xt[sl]])
                S.dma("sp", f"hxo{sl}", x_dst, xt[sl][:n, :], reads=[b_xt[sl]])

            cnt = 0
            for (off, L, kind) in seqs:
                if kind == 0:
                    S.op("pool", lambda e: e.memset(Sh, 0.0), writes=[b_S])
                    S.op("pool", lambda e: e.memset(Shb, 0.0), writes=[b_Sb])
                else:
                    S.dma("sp", "st", Sh, st_hgrn[kind - 1].rearrange("h k v -> k h v"), writes=[b_S])
                    S.op("act", lambda e: e.activation(out=Shb, in_=Sh, func=AF.Copy), reads=[b_S], writes=[b_Sb])
                nt = max(1, L // 128)
                n = min(L, 128)
                sls = []
                for t in range(nt):
                    sls.append(cnt % 2)
                    cnt += 1
                interleave([phaseA(n, xb[off:off + n, :], sls[0])])
                for t in range(nt):
                    gens = [phaseB(n, xc[off + t * n:off + (t + 1) * n, :], sls[t])]
                    if t + 1 < nt:
                        gens.insert(0, phaseA(n, xb[off + (t + 1) * n:off + (t + 2) * n, :], sls[t + 1]))
                    interleave(gens)
                S.dma("sp", "so", o_hgrn[kind].rearrange("h k v -> k h v"), Sh, reads=[b_S])

        stage_ssd()
        S.barrier()
        stage_oproj()
        S.barrier()
        stage_ffn(0, xa, xb, False)
        S.barrier()
        stage_hgrn()
        S.barrier()
        stage_ffn(1, xc, None, True)
        S.finish()
        build.last = (dict(S.cnt), dict(S.dcnt), {k: len(v) for k, v in S.q.items()}, dict(S.cur))

        with nc.Block() as block:
            @block.tensor
            def _(e):
                for f in S.q["pe"]:
                    f(e)

            @block.scalar
            def _(e):
                for f in S.q["act"]:
                    f(e)

            @block.vector
            def _(e):
                for f in S.q["dve"]:
                    f(e)

            @block.gpsimd
            def _(e):
                for f in S.q["pool"]:
                    f(e)

            @block.sync
            def _(e):
                for f in S.q["sp"]:
                    f(e)
    return nc


def _consts():
    k = np.arange(128)[:, None]
    t = np.arange(128)[None, :]
    ident = (k == t).astype(np.float32)
    triu = (k <= t).astype(np.float32)
    strictl = (k > t).astype(np.float32)
    ones = np.ones((128, 128), np.float32)
    return np.ascontiguousarray(np.concatenate([ident, triu, strictl, ones], axis=1))


def _pk(v, nblk):
    return np.ascontiguousarray(np.asarray(v, np.float32).reshape(nblk, 128).T)


def make_in_maps(inp, Tp, n_cores=8):
    f = lambda a: np.ascontiguousarray(np.asarray(a, np.float32))
    shared = {
        "consts": _consts(),
        "w_in": f(inp["ssd_in_w"][0]), "w_out": f(inp["ssd_out_w"][0]),
        "h_in": f(inp["hgrn_in_w"][0]), "h_out": f(inp["hgrn_out_w"][0]),
        "f_gate": f(inp["ffn_w_gate"]), "f_up": f(inp["ffn_w_up"]), "f_down": f(inp["ffn_w_down"]),
        "nw_all": np.ascontiguousarray(np.concatenate([
            _pk(inp["ssd_norm_w"][0], 8), _pk(inp["ffn_norm_w"][0], 8),
            _pk(inp["hgrn_norm_w"][0], 8), _pk(inp["ffn_norm_w"][1], 8)], axis=1)),
        "convw": np.ascontiguousarray(np.concatenate(
            [_pk(inp["ssd_conv_w"][0][tap], 24) for tap in range(4)] + [_pk(inp["ssd_conv_b"][0], 24)], axis=1)),
        "convb_row": f(inp["ssd_conv_b"][0])[None, :],
        "headp": np.ascontiguousarray(np.concatenate([f(inp["ssd_dt_bias"][0]), f(inp["ssd_A_log"][0]), f(inp["ssd_D"][0])])[None, :]),
        "gnw": _pk(inp["ssd_gnorm_w"][0], 16),
        "lbT": np.ascontiguousarray(np.concatenate([_pk(inp["hgrn_lower_bounds"][0], 8), _pk(inp["hgrn_lower_bounds"][1], 8)], axis=1)),
        "hgn": f(inp["hgrn_gnorm_w"][0])[None, :],
        "fnw": f(inp["final_norm_w"])[None, :],
    }
    maps = []
    xpr = f(inp["x_prompt"])
    xs = f(inp["x_sample"])
    sss = f(inp["state_ssd"][0])
    scv = f(inp["cache_conv"][0])
    shg = f(inp["state_hgrn"][0])
    for c in range(n_cores):
        m = dict(shared)
        m["xp"] = np.ascontiguousarray(xpr[c % xpr.shape[0], :Tp])
        sl = slice(NSAMP * c, NSAMP * (c + 1))
        m["xsm"] = np.ascontiguousarray(xs[sl].reshape(NSAMP * LS, D))
        m["st_ssd"] = np.ascontiguousarray(sss[sl].reshape(NSAMP, DI, NS))
        m["st_conv"] = np.ascontiguousarray(scv[sl].reshape(NSAMP, 3, 24, 128).transpose(0, 3, 2, 1).reshape(NSAMP, 128, 72))
        m["st_hgrn"] = np.ascontiguousarray(shg[sl])
        maps.append(m)
    return maps


_NC_CACHE = {}


def run(inp, Tp, n_cores=8):
    if Tp not in _NC_CACHE:
        _NC_CACHE[Tp] = build(Tp)
    nc = _NC_CACHE[Tp]
    maps = make_in_maps(inp, Tp, n_cores)
    res = run_bass_kernel_spmd(nc, maps, core_ids=list(range(n_cores)))
    return res.results


def assemble(r, Tp, n_cores=8):
    B = 2
    y_p = np.stack([r[b]["y_p"] for b in range(B)])
    y_s = np.concatenate([r[c]["y_s"].reshape(NSAMP, LS, D) for c in range(n_cores)])

    def conv_back(a):
        return a.reshape(128, 24, 3).transpose(2, 1, 0).reshape(3, CONV)
    ssd_p = np.stack([r[b]["o_ssd"][0].reshape(NH, HP, NS) for b in range(B)])[None]
    conv_p = np.stack([conv_back(r[b]["o_conv"][0]) for b in range(B)])[None]
    hgrn_p = np.stack([r[b]["o_hgrn"][0] for b in range(B)])[None]
    ssd_s = np.concatenate([r[c]["o_ssd"][1:].reshape(NSAMP, NH, HP, NS) for c in range(n_cores)])[None]
    conv_s = np.stack([conv_back(r[c]["o_conv"][1 + i]) for c in range(n_cores) for i in range(NSAMP)])[None]
    hgrn_s = np.concatenate([r[c]["o_hgrn"][1:] for c in range(n_cores)])[None]
    outs = (y_p, y_s, ssd_p, conv_p, hgrn_p, ssd_s, conv_s, hgrn_s)
    return tuple(np.ascontiguousarray(o, dtype=np.float32) for o in outs)


def kernel(**inputs):
    Tp = inputs["x_prompt"].shape[1]
    r = run(inputs, Tp)
    return assemble(r, Tp)
```
